# Optimizing a Trainium2 kernel written in Bass

```python
import jax
import jax.numpy as jnp
from jax import lax
import numpy as np

D_MODEL = 1024
BATCH = 8
SEQ = 4096
DEPTH = 4

HEAD_DIM = 64
D_MIX = D_MODEL
N_MIXERS = 4
GROUP_HEADS = D_MIX // (N_MIXERS * HEAD_DIM)
GROUP_W = GROUP_HEADS * HEAD_DIM
QBLK = 128
EPS = 1e-6

DIL_PAIRS = ((128, 1), (512, 4), (2048, 16))

NSA_KV = 2
NSA_CMP_L = 32
NSA_CMP_D = 16
NSA_CMP_HID = 2 * HEAD_DIM
NSA_SLC_L = 64
NSA_TOP_N = 16
NSA_WIN = 512
NSA_QC = 32
NSA_FORCE = 1e4

GLA_DK = HEAD_DIM // 2
GLA_RANK = 16
GLA_TAU = 16.0
GLA_CHUNK = 64

SWA_WIN = 128
SWA_KV = 1

IN_SPLITS = (
    ('a_q', GROUP_W), ('a_k', GROUP_W), ('a_v', GROUP_W), ('a_z', GROUP_W),
    ('b_q', GROUP_W), ('b_kc', NSA_KV * HEAD_DIM), ('b_vc', NSA_KV * HEAD_DIM),
    ('b_ks', NSA_KV * HEAD_DIM), ('b_vs', NSA_KV * HEAD_DIM),
    ('b_kw', NSA_KV * HEAD_DIM), ('b_vw', NSA_KV * HEAD_DIM),
    ('b_g', GROUP_HEADS * 3), ('b_z', GROUP_W),
    ('c_q', GROUP_HEADS * GLA_DK), ('c_k', GROUP_HEADS * GLA_DK), ('c_v', GROUP_W),
    ('c_a', GLA_RANK), ('c_z', GROUP_W),
    ('d_q', GROUP_W), ('d_k', SWA_KV * HEAD_DIM), ('d_v', SWA_KV * HEAD_DIM), ('d_z', GROUP_W),
)
D_IN = sum(w for _, w in IN_SPLITS)

kernel_name = 'hybrid_dilated_nsa_gla_swa_block'


def rms_norm(x, g):
    xf = x.astype(jnp.float32)
    y = xf * lax.rsqrt(jnp.mean(xf * xf, axis=-1, keepdims=True) + EPS)
    return (y * g.astype(jnp.float32)).astype(x.dtype)


def split_columns(u):
    out = {}
    off = 0
    for name, w in IN_SPLITS:
        out[name] = u[..., off:off + w]
        off += w
    return out


def banded_attention(q, k, v, back, sink=None):
    n, L, hq, dh = q.shape
    hkv = k.shape[2]
    g = hq // hkv
    blk = min(QBLK, L)
    nq = -(-L // blk)
    lp = nq * blk
    nb = -(-back // blk)
    span = (nb + 1) * blk
    pad_end = ((0, 0), (0, lp - L), (0, 0), (0, 0))
    q = jnp.pad(q, pad_end)
    pad_kv = ((0, 0), (nb * blk, lp - L), (0, 0), (0, 0))
    kp = jnp.pad(k, pad_kv).reshape(n, nq + nb, blk, hkv, dh)
    vp = jnp.pad(v, pad_kv).reshape(n, nq + nb, blk, hkv, dh)
    bidx = jnp.arange(nq)[:, None] + jnp.arange(nb + 1)[None, :]
    kb = kp[:, bidx].reshape(n, nq, span, hkv, dh)
    vb = vp[:, bidx].reshape(n, nq, span, hkv, dh)
    qb = q.reshape(n, nq, blk, hkv, g, dh)
    s = jnp.einsum('bnqhgd,bnkhd->bnhgqk', qb, kb, preferred_element_type=jnp.float32) * (dh ** -0.5)
    qpos = jnp.arange(lp).reshape(nq, blk)
    kpos = (jnp.arange(nq)[:, None] - nb) * blk + jnp.arange(span)[None, :]
    diff = qpos[:, :, None] - kpos[:, None, :]
    mask = (diff >= 0) & (diff <= back) & (kpos[:, None, :] >= 0)
    s = jnp.where(mask[None, :, None, None], s, -jnp.inf)
    m = jnp.max(s, axis=-1, keepdims=True)
    if sink is not None:
        sk = sink.astype(jnp.float32).reshape(hkv, g)[None, None, :, :, None, None]
        m = jnp.maximum(m, sk)
    p = jnp.exp(s - m)
    l = jnp.sum(p, axis=-1, keepdims=True)
    denom = l if sink is None else l + jnp.exp(sk - m)
    o = jnp.einsum('bnhgqk,bnkhd->bnhgqd', p, vb.astype(jnp.float32)) / denom
    o = o.transpose(0, 1, 4, 2, 3, 5).reshape(n, lp, hq, dh)[:, :L]
    lse = (m + jnp.log(l))[..., 0].transpose(0, 1, 4, 2, 3).reshape(n, lp, hq)[:, :L]
    return o, lse


def dilated_attention(q, k, v):
    b, s, h, dh = q.shape
    outs, lses = [], []
    for w, r in DIL_PAIRS:
        def to_res(t):
            return t.reshape(b, s // r, r, h, dh).transpose(0, 2, 1, 3, 4).reshape(b * r, s // r, h, dh)
        o, lse = banded_attention(to_res(q), to_res(k), to_res(v), w // r)
        outs.append(o.reshape(b, r, s // r, h, dh).transpose(0, 2, 1, 3, 4).reshape(b, s, h, dh))
        lses.append(lse.reshape(b, r, s // r, h).transpose(0, 2, 1, 3).reshape(b, s, h))
    wts = jax.nn.softmax(jnp.stack(lses, axis=-1), axis=-1)
    return jnp.einsum('bshp,pbshd->bshd', wts, jnp.stack(outs))


def nsa_attention(q, kc, vc, ks, vs, kw, vw, gate_logits, gate_b,
                  pos_k, pos_v, w1k, b1k, w2k, w1v, b1v, w2v):
    b, s, hq, dh = q.shape
    hkv = kc.shape[2]
    g = hq // hkv
    scale = dh ** -0.5
    t = jnp.arange(s)
    qg = q.reshape(b, s, hkv, g, dh)

    n_cmp = (s - NSA_CMP_L) // NSA_CMP_D + 1
    tok = jnp.arange(n_cmp)[:, None] * NSA_CMP_D + jnp.arange(NSA_CMP_L)[None, :]

    def compress(xk, pos, w1, b1, w2):
        xb = xk[:, tok] + pos[None, None, :, None, :]
        xb = xb.transpose(0, 1, 3, 2, 4).reshape(b, n_cmp, hkv, NSA_CMP_L * dh)
        return jax.nn.gelu(xb @ w1 + b1) @ w2

    k_cmp = compress(kc, pos_k, w1k, b1k, w2k)
    v_cmp = compress(vc, pos_v, w1v, b1v, w2v)
    s_c = jnp.einsum('bthgd,bihd->bhgti', qg, k_cmp, preferred_element_type=jnp.float32) * scale
    cmp_end = jnp.arange(n_cmp) * NSA_CMP_D + NSA_CMP_L - 1
    cmask = cmp_end[None, :] <= t[:, None]
    s_c = jnp.where(cmask, s_c, -jnp.inf)
    m_c = jnp.max(s_c, axis=-1, keepdims=True)
    m_c = jnp.where(jnp.isfinite(m_c), m_c, 0.0)
    e_c = jnp.exp(s_c - m_c)
    p_c = e_c / jnp.maximum(jnp.sum(e_c, axis=-1, keepdims=True), 1e-30)
    o_cmp = jnp.einsum('bhgti,bihd->bthgd', p_c, v_cmp.astype(jnp.float32))

    n_slc = s // NSA_SLC_L
    slc_start = jnp.arange(n_slc) * NSA_SLC_L
    cmp_start = jnp.arange(n_cmp) * NSA_CMP_D
    overlap = ((cmp_start[:, None] < slc_start[None, :] + NSA_SLC_L) &
               (cmp_start[:, None] + NSA_CMP_L > slc_start[None, :])).astype(jnp.float32)
    imp = jnp.einsum('bhgti,ij->bhtj', p_c, overlap)
    cur = t // NSA_SLC_L
    j = jnp.arange(n_slc)
    forced = (j[None, :] == 0) | (j[None, :] == cur[:, None]) | (j[None, :] == cur[:, None] - 1)
    causal = j[None, :] <= cur[:, None]
    score = jnp.where(forced, NSA_FORCE, jnp.where(causal, imp, -NSA_FORCE))
    k_eff = min(NSA_TOP_N, n_slc)
    _, sel = lax.top_k(score, k_eff)

    ks_b = ks.reshape(b, n_slc, NSA_SLC_L, hkv, dh).transpose(0, 3, 1, 2, 4)
    vs_b = vs.reshape(b, n_slc, NSA_SLC_L, hkv, dh).transpose(0, 3, 1, 2, 4)
    qc = min(NSA_QC, s)
    nchunk = s // qc
    q_ch = qg.reshape(b, nchunk, qc, hkv, g, dh).transpose(1, 0, 2, 3, 4, 5)
    sel_ch = sel.reshape(b, hkv, nchunk, qc, k_eff).transpose(2, 0, 1, 3, 4)
    t_ch = t.reshape(nchunk, qc)
    bi = jnp.arange(b)[:, None, None, None]
    hi = jnp.arange(hkv)[None, :, None, None]
    kk = jnp.arange(NSA_SLC_L)

    def one_chunk(args):
        qx, sx, tx = args
        kx = ks_b[bi, hi, sx]
        vx = vs_b[bi, hi, sx]
        sc = jnp.einsum('bqhgd,bhqnkd->bhgqnk', qx, kx, preferred_element_type=jnp.float32) * scale
        kpos = sx[..., None] * NSA_SLC_L + kk
        valid = (kpos <= tx[None, None, :, None, None])[:, :, None]
        sc = jnp.where(valid, sc, -jnp.inf).reshape(b, hkv, g, qc, k_eff * NSA_SLC_L)
        p = jax.nn.softmax(sc, axis=-1)
        return jnp.einsum('bhgqm,bhqmd->bqhgd', p,
                          vx.reshape(b, hkv, qc, k_eff * NSA_SLC_L, dh).astype(jnp.float32))

    o_slc = lax.map(one_chunk, (q_ch, sel_ch, t_ch))
    o_slc = o_slc.transpose(1, 0, 2, 3, 4, 5).reshape(b, s, hq, dh)

    o_win, _ = banded_attention(q, kw, vw, NSA_WIN - 1)

    gates = jax.nn.sigmoid((gate_logits + gate_b).astype(jnp.float32)).reshape(b, s, hq, 3)
    return (gates[..., 0:1] * o_cmp.reshape(b, s, hq, dh) + gates[..., 1:2] * o_slc
            + gates[..., 2:3] * o_win)


def gla_attention(q, k, v, log_a):
    b, s, h, dk = q.shape
    dv = v.shape[-1]
    C = min(GLA_CHUNK, s)
    n = s // C

    def chunks(z):
        return z.reshape(b, n, C, h, z.shape[-1]).transpose(1, 0, 3, 2, 4).astype(jnp.float32)

    qc = chunks(q) * (dk ** -0.5)
    kc, vc = chunks(k), chunks(v)
    bcum = jnp.cumsum(chunks(log_a), axis=3)
    causal = jnp.tril(jnp.ones((C, C), dtype=bool))[..., None]

    def step(state, xs):
        qx, kx, vx, bx = xs
        decay = jnp.exp(jnp.where(causal, bx[:, :, :, None, :] - bx[:, :, None, :, :], -jnp.inf))
        att = jnp.einsum('bhid,bhjd,bhijd->bhij', qx, kx, decay)
        o = (jnp.einsum('bhij,bhjv->bhiv', att, vx)
             + jnp.einsum('bhid,bhdv->bhiv', qx * jnp.exp(bx), state))
        blast = bx[:, :, -1:, :]
        state = (jnp.exp(blast[:, :, 0, :, None]) * state
                 + jnp.einsum('bhjd,bhjv->bhdv', kx * jnp.exp(blast - bx), vx))
        return state, o

    state0 = jnp.zeros((b, h, dk, dv), jnp.float32)
    _, o = lax.scan(step, state0, (qc, kc, vc, bcum))
    return o.transpose(1, 0, 3, 2, 4).reshape(b, s, h, dv)


def hybrid_layer(x, c, ln_pre, ln_post, w_ada, b_ada, w_in, w_out,
                 nsa_pos_k, nsa_pos_v, nsa_w1_k, nsa_b1_k, nsa_w2_k,
                 nsa_w1_v, nsa_b1_v, nsa_w2_v, nsa_gate_b,
                 gla_w_a2, gla_b_a, gla_norm, sinks):
    b, s, _ = x.shape
    shift, scale, gate = jnp.split(jax.nn.silu(c) @ w_ada + b_ada, 3, axis=-1)
    h = rms_norm(x, ln_pre) * (1.0 + scale[:, None]) + shift[:, None]
    u = split_columns(h @ w_in)

    def heads(z, nh):
        return z.reshape(b, s, nh, -1)

    o_a = dilated_attention(heads(u['a_q'], GROUP_HEADS), heads(u['a_k'], GROUP_HEADS),
                            heads(u['a_v'], GROUP_HEADS))
    o_b = nsa_attention(heads(u['b_q'], GROUP_HEADS),
                        heads(u['b_kc'], NSA_KV), heads(u['b_vc'], NSA_KV),
                        heads(u['b_ks'], NSA_KV), heads(u['b_vs'], NSA_KV),
                        heads(u['b_kw'], NSA_KV), heads(u['b_vw'], NSA_KV),
                        u['b_g'], nsa_gate_b, nsa_pos_k, nsa_pos_v,
                        nsa_w1_k, nsa_b1_k, nsa_w2_k, nsa_w1_v, nsa_b1_v, nsa_w2_v)
    log_a = jax.nn.log_sigmoid(u['c_a'] @ gla_w_a2 + gla_b_a) / GLA_TAU
    o_c = gla_attention(heads(u['c_q'], GROUP_HEADS), heads(u['c_k'], GROUP_HEADS),
                        heads(u['c_v'], GROUP_HEADS), heads(log_a, GROUP_HEADS))
    o_c = rms_norm(o_c, gla_norm.reshape(GROUP_HEADS, HEAD_DIM))
    o_d, _ = banded_attention(heads(u['d_q'], GROUP_HEADS), heads(u['d_k'], SWA_KV),
                              heads(u['d_v'], SWA_KV), SWA_WIN - 1, sinks)

    mixed = jnp.concatenate([
        o_a.reshape(b, s, GROUP_W).astype(x.dtype) * jax.nn.silu(u['a_z']),
        o_b.reshape(b, s, GROUP_W).astype(x.dtype) * jax.nn.silu(u['b_z']),
        o_c.reshape(b, s, GROUP_W).astype(x.dtype) * jax.nn.silu(u['c_z']),
        o_d.reshape(b, s, GROUP_W).astype(x.dtype) * jax.nn.silu(u['d_z']),
    ], axis=-1)
    y = mixed @ w_out
    return x + gate[:, None] * rms_norm(y, ln_post)


def setup_inputs(seed: int = 0) -> dict:
    key = jax.random.key(seed)
    ks = jax.random.split(key, 24)
    f32 = jnp.float32

    def nrm(k, shape, sc):
        return jax.random.normal(k, shape, f32) * sc

    cmp_in = NSA_CMP_L * HEAD_DIM
    return {
        'x': nrm(ks[0], (BATCH, SEQ, D_MODEL), 1.0),
        'c': nrm(ks[1], (BATCH, D_MODEL), 1.0),
        'ln_pre': 1.0 + nrm(ks[2], (DEPTH, D_MODEL), 0.02),
        'ln_post': 1.0 + nrm(ks[3], (DEPTH, D_MODEL), 0.02),
        'w_ada': nrm(ks[4], (DEPTH, D_MODEL, 3 * D_MODEL), 0.5 * D_MODEL ** -0.5),
        'b_ada': nrm(ks[5], (DEPTH, 3 * D_MODEL), 0.02),
        'w_in': nrm(ks[6], (DEPTH, D_MODEL, D_IN), D_MODEL ** -0.5),
        'w_out': nrm(ks[7], (DEPTH, D_MIX, D_MODEL), D_MIX ** -0.5),
        'nsa_pos_k': nrm(ks[8], (DEPTH, NSA_CMP_L, HEAD_DIM), 0.02),
        'nsa_pos_v': nrm(ks[9], (DEPTH, NSA_CMP_L, HEAD_DIM), 0.02),
        'nsa_w1_k': nrm(ks[10], (DEPTH, cmp_in, NSA_CMP_HID), cmp_in ** -0.5),
        'nsa_b1_k': nrm(ks[11], (DEPTH, NSA_CMP_HID), 0.02),
        'nsa_w2_k': nrm(ks[12], (DEPTH, NSA_CMP_HID, HEAD_DIM), NSA_CMP_HID ** -0.5),
        'nsa_w1_v': nrm(ks[13], (DEPTH, cmp_in, NSA_CMP_HID), cmp_in ** -0.5),
        'nsa_b1_v': nrm(ks[14], (DEPTH, NSA_CMP_HID), 0.02),
        'nsa_w2_v': nrm(ks[15], (DEPTH, NSA_CMP_HID, HEAD_DIM), NSA_CMP_HID ** -0.5),
        'nsa_gate_b': nrm(ks[16], (DEPTH, GROUP_HEADS * 3), 0.02),
        'gla_w_a2': nrm(ks[17], (DEPTH, GLA_RANK, GROUP_HEADS * GLA_DK), GLA_RANK ** -0.5),
        'gla_b_a': nrm(ks[18], (DEPTH, GROUP_HEADS * GLA_DK), 0.5),
        'gla_norm': 1.0 + nrm(ks[19], (DEPTH, GROUP_W), 0.02),
        'sinks': nrm(ks[20], (DEPTH, GROUP_HEADS), 0.5),
    }


def reference(x, c, ln_pre, ln_post, w_ada, b_ada, w_in, w_out,
              nsa_pos_k, nsa_pos_v, nsa_w1_k, nsa_b1_k, nsa_w2_k,
              nsa_w1_v, nsa_b1_v, nsa_w2_v, nsa_gate_b,
              gla_w_a2, gla_b_a, gla_norm, sinks):
    for i in range(DEPTH):
        x = hybrid_layer(x, c, ln_pre[i], ln_post[i], w_ada[i], b_ada[i], w_in[i], w_out[i],
                         nsa_pos_k[i], nsa_pos_v[i], nsa_w1_k[i], nsa_b1_k[i], nsa_w2_k[i],
                         nsa_w1_v[i], nsa_b1_v[i], nsa_w2_v[i], nsa_gate_b[i],
                         gla_w_a2[i], gla_b_a[i], gla_norm[i], sinks[i])
    return x
```

```python
from contextlib import ExitStack
import numpy as np
import ml_dtypes
import concourse.bass as bass
import concourse.mybir as mybir
from concourse.bass_utils import run_bass_kernel_spmd

F32 = mybir.dt.float32
BF16 = mybir.dt.bfloat16
ALU = mybir.AluOpType
AF = mybir.ActivationFunctionType
AX = mybir.AxisListType

S = 4096
D = 1024
NT = 32
NCH = 8
EPS = 1e-6

IN_SPLITS = (('a_q', 256), ('a_k', 256), ('a_v', 256), ('a_z', 256),
             ('b_q', 256), ('b_kc', 128), ('b_vc', 128), ('b_ks', 128), ('b_vs', 128),
             ('b_kw', 128), ('b_vw', 128), ('b_g', 12), ('b_z', 256),
             ('c_q', 128), ('c_k', 128), ('c_v', 256), ('c_a', 16), ('c_z', 256),
             ('d_q', 256), ('d_k', 64), ('d_v', 64), ('d_z', 256))
OFF = {}
_o = 0
for _n, _w in IN_SPLITS:
    OFF[_n] = _o
    _o += _w


class R:
    __slots__ = ("name", "w", "rd", "dsem", "dcount")

    def __init__(self, name):
        self.name = name
        self.w = None
        self.rd = []
        self.dsem = None
        self.dcount = 0


class Op:
    __slots__ = ("eng", "fn", "waits", "inc", "pos", "val", "dma")

    def __init__(self, eng, fn, pos):
        self.eng = eng
        self.fn = fn
        self.waits = []
        self.inc = False
        self.pos = pos
        self.val = None
        self.dma = None


class Eng:
    def __init__(self, name, sem):
        self.name = name
        self.sem = sem
        self.ops = []
        self.waited_pos = {}
        self.waited_dma = {}
        self.final = 0


class Prog:
    ENGS = ("pe", "act", "dve", "pool", "sp")

    def __init__(self, nc):
        self.nc = nc
        self.stack = ExitStack()
        self.engs = {}
        for n in self.ENGS:
            sem = self.stack.enter_context(nc.semaphore("es_" + n))
            self.engs[n] = Eng(n, sem)
        self.all_dma = {}
        self.nsem = 0

    def sbuf(self, name, shape, dtype):
        return self.stack.enter_context(self.nc.sbuf_tensor("sb_" + name, list(shape), dtype))

    def psum(self, name, shape, dtype):
        return self.stack.enter_context(self.nc.psum_tensor("ps_" + name, list(shape), dtype))

    def _dsem(self, r):
        if r.dsem is None:
            self.nsem += 1
            r.dsem = self.stack.enter_context(self.nc.semaphore("ds%d" % self.nsem))
        return r.dsem

    def _need(self, op, tok, same_eng_ok):
        if tok is None:
            return
        E = self.engs[op.eng]
        if tok[0] == "op":
            p = tok[1]
            if p.eng == op.eng and (not same_eng_ok or op.eng == "pe"):
                return
            if E.waited_pos.get(p.eng, -1) >= p.pos:
                return
            E.waited_pos[p.eng] = p.pos
            p.inc = True
            op.waits.append(tok)
        else:
            _, sem, val = tok
            if E.waited_dma.get(sem, 0) >= val:
                return
            E.waited_dma[sem] = val
            op.waits.append(tok)

    def _deps(self, op, reads, writes):
        for r in reads:
            self._need(op, r.w, True)
        for r in writes:
            self._need(op, r.w, True)
            for t in r.rd:
                self._need(op, t, False)

    def op(self, eng, fn, reads=(), writes=()):
        E = self.engs[eng]
        o = Op(eng, fn, len(E.ops))
        self._deps(o, reads, writes)
        E.ops.append(o)
        tok = ("op", o)
        for r in reads:
            r.rd.append(tok)
        for r in writes:
            r.w = tok
            r.rd = []
        return o

    def dma(self, eng, out, in_, reads=(), writes=(), semr=None, **kw):
        E = self.engs[eng]
        if semr is None:
            semr = writes[0] if writes else reads[0]
        sem = self._dsem(semr)
        semr.dcount += 16
        val = semr.dcount
        o = Op(eng, lambda e: e.dma_start(out=out, in_=in_, **kw), len(E.ops))
        o.dma = (sem, val)
        self._deps(o, reads, writes)
        E.ops.append(o)
        tok = ("dma", sem, val)
        self.all_dma[sem] = val
        for r in reads:
            r.rd.append(tok)
        for r in writes:
            r.w = tok
            r.rd = []
        return o

    def emit(self):
        nc = self.nc
        engs = self.engs
        for E in engs.values():
            if E.name != "sp" and E.ops:
                last = E.ops[-1]
                if last.dma is None:
                    last.inc = True
        for E in engs.values():
            n = 0
            for o in E.ops:
                if o.inc:
                    n += 1
                    o.val = n
            E.final = n
        final_waits = []
        for E in engs.values():
            if E.name != "sp" and E.final > 0:
                final_waits.append((E.sem, E.final))
        for sem, val in self.all_dma.items():
            final_waits.append((sem, val))

        def replay(E, e):
            for o in E.ops:
                for t in o.waits:
                    if t[0] == "op":
                        p = t[1]
                        e.wait_ge(engs[p.eng].sem, p.val)
                    else:
                        e.wait_ge(t[1], t[2])
                ins = o.fn(e)
                if o.dma is not None:
                    ins.then_inc(o.dma[0], 16)
                elif o.inc:
                    ins.then_inc(E.sem, 1)

        with nc.Block() as block:
            @block.tensor
            def _(e):
                replay(engs["pe"], e)

            @block.scalar
            def _(e):
                replay(engs["act"], e)

            @block.vector
            def _(e):
                replay(engs["dve"], e)

            @block.gpsimd
            def _(e):
                replay(engs["pool"], e)

            @block.sync
            def _(e):
                replay(engs["sp"], e)
                for sem, val in final_waits:
                    e.wait_ge(sem, val)

    def close(self):
        self.stack.close()

    def stats(self):
        return {n: (len(E.ops), sum(1 for o in E.ops if o.inc), sum(len(o.waits) for o in E.ops))
                for n, E in self.engs.items()}


def _wl_blocks():
    bl = []

    def rng(name, a, b):
        return list(range(OFF[name] + a, OFF[name] + b))
    for p in range(2):
        bl.append(("A_q%d" % p, rng('a_q', 128 * p, 128 * p + 128)))
        bl.append(("A_k%d" % p, rng('a_k', 128 * p, 128 * p + 128)))
        bl.append(("A_v%d" % p, rng('a_v', 128 * p, 128 * p + 128)))
        bl.append(("A_z%d" % p, rng('a_z', 128 * p, 128 * p + 128)))
    bl.append(("B_qX", rng('b_q', 0, 64) + rng('b_q', 128, 192)))
    bl.append(("B_qY", rng('b_q', 64, 128) + rng('b_q', 192, 256)))
    for n in ("kc", "ks", "kw", "vc", "vs", "vw"):
        bl.append(("B_" + n, rng('b_' + n, 0, 128)))
    for t, hs in (("X", (0, 2)), ("Y", (1, 3))):
        for br in range(3):
            bl.append(("B_g%s%d" % (t, br), [OFF['b_g'] + hs[0] * 3 + br] * 64 + [OFF['b_g'] + hs[1] * 3 + br] * 64))
    bl.append(("B_zX", rng('b_z', 0, 64) + rng('b_z', 128, 192)))
    bl.append(("B_zY", rng('b_z', 64, 128) + rng('b_z', 192, 256)))
    bl.append(("C_q", rng('c_q', 0, 128)))
    bl.append(("C_k", rng('c_k', 0, 128)))
    bl.append(("C_v0", rng('c_v', 0, 128)))
    bl.append(("C_v1", rng('c_v', 128, 256)))
    bl.append(("C_a", rng('c_a', 0, 16) + [-1] * 16))
    bl.append(("C_z0", rng('c_z', 0, 128)))
    bl.append(("C_z1", rng('c_z', 128, 256)))
    for p in range(2):
        bl.append(("D_q%d" % p, rng('d_q', 128 * p, 128 * p + 128)))
    bl.append(("D_k", rng('d_k', 0, 64) + rng('d_k', 0, 64)))
    bl.append(("D_v", rng('d_v', 0, 64)))
    for p in range(2):
        bl.append(("D_z%d" % p, rng('d_z', 128 * p, 128 * p + 128)))
    return bl


WL_BLOCKS = _wl_blocks()
WL_OFF = {}
_o = 0
for _n, _c in WL_BLOCKS:
    WL_OFF[_n] = (_o, len(_c))
    _o += len(_c)
NW = _o

MIX_ROWS = (list(range(0, 128)), list(range(128, 256)),
            list(range(256, 320)) + list(range(384, 448)), list(range(320, 384)) + list(range(448, 512)),
            list(range(512, 640)), list(range(640, 768)), list(range(768, 896)), list(range(896, 1024)))
MIX_A0, MIX_A1, MIX_BX, MIX_BY, MIX_C0, MIX_C1, MIX_D0, MIX_D1 = range(8)

CB = {}
_o = 0
for _n, _w in (("ident", 128), ("m_ge", 128), ("m_le", 128), ("m_lt", 128), ("stair", 2560),
               ("onespad", 192), ("ov1", 130), ("bdm4", 512)):
    CB[_n] = (_o, _w)
    _o += _w
NCB = _o
CF = {}
_o = 0
for _n, _w in (("patC", 128), ("patA", 128), ("tri2", 128), ("utri2", 128), ("ones", 128), ("zm", 256), ("rm", 2)):
    CF[_n] = (_o, _w)
    _o += _w
NCF = _o
PF = {}
_o = 0
for _n, _w in (("lnpre", 8), ("bada", 24), ("b1k", 1), ("b1v", 1), ("gb", 6), ("sink", 2), ("posk", 32), ("posv", 32),
               ("w2k", 128), ("w2v", 128), ("wa2", 128)):
    PF[_n] = (_o, _w)
    _o += _w
NPF = _o
PR = {}
_o = 0
for _n, _w in (("ba", 128), ("lnpost", 1024), ("glanorm", 256), ("badag", 1024)):
    PR[_n] = (_o, _w)
    _o += _w
NPR = _o


def _consts():
    k = np.arange(128)[:, None]
    q = np.arange(128)[None, :]
    cb = np.zeros((128, NCB), np.float32)

    def put(name, arr):
        o, w = CB[name]
        cb[:, o:o + w] = arr
    put("ident", np.eye(128))
    put("m_ge", (q >= k))
    put("m_le", (q <= k))
    put("m_lt", (q < k))
    tau = np.arange(2560)[None, :]
    put("stair", (tau - 16 * k - 31 >= 0))
    put("onespad", np.concatenate([np.ones((128, 64)), np.zeros((128, 64)), np.ones((128, 64))], axis=1))
    ov = np.zeros((128, 130))
    for kt in range(2):
        i = (kt * 128 + np.arange(128))[:, None]
        j = np.arange(64)[None, :]
        ov[:, kt * 65:kt * 65 + 64] = ((16 * i < 64 * j + 64) & (16 * i + 32 > 64 * j))
        ov[:, kt * 65 + 64] = 1.0
    put("ov1", ov)
    bd = ((k // 64) == (q // 64)) & (k <= q)
    put("bdm4", np.concatenate([bd] * 4, axis=1))
    cf = np.zeros((128, NCF), np.float32)

    def putf(name, arr):
        o, w = CF[name]
        cf[:, o:o + w] = arr
    s = (k >= 64).astype(np.int64)
    u = q
    putf("patC", (u < 63 + s))
    pa = np.zeros((128, 128))
    pa[(u == 63 + s)] = 1.1e4
    pa[(u == 64 + s)] = 1.2e4
    pa[(u > 64 + s)] = -1e4
    putf("patA", pa)
    putf("tri2", ((k // 64) == (q // 64)) & (k <= q))
    putf("utri2", ((k // 64) == (q // 64)) & (k > q))
    putf("ones", np.ones((128, 128)))
    putf("zm", ((np.arange(128)[:, None] // 32) == (np.arange(256)[None, :] // 64)))
    rm0 = ((np.arange(128) // 32) % 2 == 0)
    putf("rm", np.stack([rm0, ~rm0], axis=1))
    j = (np.arange(128) % 64)[:, None]
    kk = np.arange(4096)[None, :]
    cE = ((kk // 64) == j).astype(np.float32)
    return cb.astype(ml_dtypes.bfloat16), cf, cE.astype(ml_dtypes.bfloat16)


def _host_layouts(inp):
    nl = inp['w_in'].shape[0]
    w_in = inp['w_in']
    wl = np.zeros((nl, D, NW), np.float32)
    for name, cols in WL_BLOCKS:
        o, w = WL_OFF[name]
        cols = np.asarray(cols)
        valid = cols >= 0
        wl[:, :, o + np.nonzero(valid)[0]] = w_in[:, :, cols[valid]]
    perm = np.concatenate([np.asarray(r) for r in MIX_ROWS])
    wol = np.ascontiguousarray(inp['w_out'][:, perm, :])
    pf = np.zeros((nl, 128, NPF), np.float32)

    def put(name, arr):
        o, w = PF[name]
        pf[:, :, o:o + w] = arr
    put("lnpre", inp['ln_pre'].reshape(nl, 8, 128).transpose(0, 2, 1))
    put("bada", inp['b_ada'].reshape(nl, 24, 128).transpose(0, 2, 1))
    put("b1k", inp['nsa_b1_k'][:, :, None])
    put("b1v", inp['nsa_b1_v'][:, :, None])
    gb = inp['nsa_gate_b']
    g6 = np.zeros((nl, 128, 6), np.float32)
    for ti, hs in enumerate(((0, 2), (1, 3))):
        for br in range(3):
            g6[:, 0:64, ti * 3 + br] = gb[:, hs[0] * 3 + br][:, None]
            g6[:, 64:128, ti * 3 + br] = gb[:, hs[1] * 3 + br][:, None]
    put("gb", g6)
    sk = np.zeros((nl, 128, 2), np.float32)
    for p in range(2):
        sk[:, 0:64, p] = inp['sinks'][:, 2 * p][:, None]
        sk[:, 64:128, p] = inp['sinks'][:, 2 * p + 1][:, None]
    put("sink", sk)
    put("posk", np.concatenate([inp['nsa_pos_k'].transpose(0, 2, 1)] * 2, axis=1))
    put("posv", np.concatenate([inp['nsa_pos_v'].transpose(0, 2, 1)] * 2, axis=1))
    put("w2k", np.concatenate([inp['nsa_w2_k']] * 2, axis=2))
    put("w2v", np.concatenate([inp['nsa_w2_v']] * 2, axis=2))
    wa = np.zeros((nl, 128, 128), np.float32)
    wa[:, 0:16, :] = inp['gla_w_a2']
    put("wa2", wa)
    pr = np.zeros((nl, 1, NPR), np.float32)

    def putr(name, arr):
        o, w = PR[name]
        pr[:, 0, o:o + w] = arr
    putr("ba", inp['gla_b_a'])
    putr("lnpost", inp['ln_post'])
    putr("glanorm", inp['gla_norm'])
    putr("badag", inp['b_ada'][:, 2048:3072])

    def w1lay(w1):
        a = w1.reshape(nl, 32, 64, 128).transpose(0, 2, 1, 3)
        return np.ascontiguousarray(np.concatenate([a, a], axis=1))
    w1k = w1lay(inp['nsa_w1_k'])
    w1v = w1lay(inp['nsa_w1_v'])
    cT = np.ascontiguousarray(inp['c'].reshape(-1, 8, 128).transpose(0, 2, 1))
    return dict(wl=wl, wol=wol, pf=pf, pr=pr, w1k=w1k, w1v=w1v, wada=np.ascontiguousarray(inp['w_ada']), cT=cT)


def build_program(nl, debug=False, mixers="ABCD"):
    nc = bass.Bass("TRN2", target_bir_lowering=False)
    P = Prog(nc)

    def din(name, shape, dt=F32):
        return nc.dram_tensor(name, list(shape), dt, kind="ExternalInput").ap()
    x_in = din("x", [S, D])
    cT_d = din("cT", [128, 8])
    wl_d = din("wl", [nl, D, NW])
    wol_d = din("wol", [nl, D, D])
    wada_d = din("wada", [nl, D, 3 * D])
    pf_d = din("pf", [nl, 128, NPF])
    pr_d = din("pr", [nl, 1, NPR])
    w1_d = {"k": din("w1k", [nl, 128, 32, 128]), "v": din("w1v", [nl, 128, 32, 128])}
    cb_d = din("cb", [128, NCB], BF16)
    cf_d = din("cf", [128, NCF])
    cE_d = din("cE", [128, S], BF16)
    out_d = nc.dram_tensor("out", [S, D], F32, kind="ExternalOutput").ap()
    mixT_d = nc.dram_tensor("mixT", [8, 128, S], BF16).ap()
    if debug:
        dbg_d = nc.dram_tensor("dbg", [8, 128, S], BF16, kind="ExternalOutput").ap()

    hT = P.sbuf("hT", [128, 8, S], BF16)
    hT_r = R("hT")
    cb = P.sbuf("cb", [128, NCB], BF16)
    cf = P.sbuf("cf", [128, NCF], F32)
    cE = P.sbuf("cE", [128, S], BF16)
    c_r = R("consts")
    pf = P.sbuf("pf", [128, NPF], F32)
    pf_r = R("pf")
    prb = P.sbuf("prb", [128, 384], F32)
    pr_r = R("prb")
    G = [P.sbuf("G%d" % i, [128, S], BF16) for i in range(4)]
    G_r = [R("G%d" % i) for i in range(4)]
    Fb = [P.sbuf("F%d" % i, [128, S], F32) for i in range(2)]
    F_r = [R("F%d" % i) for i in range(2)]
    Fbf = [Fb[i].bitcast(BF16) for i in range(2)]
    V = [P.sbuf("V%d" % i, [128, NT, 192], BF16) for i in range(2)]
    V_r = [R("V%d" % i) for i in range(2)]
    wst = [P.sbuf("wst%d" % i, [128, 8, 128], F32) for i in range(2)]
    wst_r = [R("wst%d" % i) for i in range(2)]
    NWB = 2
    wbf = [P.sbuf("wbf%d" % i, [128, 8, 128], BF16) for i in range(NWB)]
    wbf_r = [R("wbf%d" % i) for i in range(NWB)]
    NPT = 4
    pt = [P.sbuf("pt%d" % i, [128, 512], BF16) for i in range(NPT)]
    pt_r = [R("pt%d" % i) for i in range(NPT)]
    NSC = 3
    sc = [P.sbuf("sc%d" % i, [128, 512], F32) for i in range(NSC)]
    sc_r = [R("sc%d" % i) for i in range(NSC)]
    NMX = 2
    mx = [P.sbuf("mx%d" % i, [128, 512], BF16) for i in range(NMX)]
    mx_r = [R("mx%d" % i) for i in range(NMX)]
    sm = P.sbuf("sm", [128, 64], F32)
    sm_r = R("sm")
    gp_t = P.sbuf("gp_t", [128, 1024], F32)
    gp_r = R("gp")
    vcmp = P.sbuf("vcmp", [128, 2, 192], BF16)
    vcmp_r = R("vcmp")
    kcmpT = P.sbuf("kcmpT", [128, 256], BF16)
    kcmp_r = R("kcmp")
    misc = P.sbuf("miscb", [128, 1408], BF16)
    misc_r = R("miscb")
    accB = [Fb[0][:, 0:512], Fb[0][:, 512:1024]]
    accB_r = [R("accB%d" % i) for i in range(2)]
    impS = [Fb[0][:, 1024:1536], Fb[0][:, 1536:2048]]
    impS_r = [R("impS%d" % i) for i in range(2)]
    pb = [P.psum("pb%d" % i, [128, 512], F32) for i in range(8)]
    pb_r = [R("pb%d" % i) for i in range(8)]

    def cbv(name):
        o, w = CB[name]
        return cb[:, o:o + w]

    def cfv(name):
        o, w = CF[name]
        return cf[:, o:o + w]

    def pfv(name):
        o, w = PF[name]
        return pf[:, o:o + w]

    rot = {"s": 0, "m": 0, "pt": 0, "sc": 0, "wst": 0, "wbf": 0, "mx": 0}

    def sbank():
        rot["s"] = (rot["s"] + 1) % 3
        return rot["s"]

    def mbank():
        rot["m"] = (rot["m"] + 1) % 3
        return 5 + rot["m"]

    def nxt(key, n):
        rot[key] = (rot[key] + 1) % n
        return rot[key]

    def mm(out, lhsT, rhs, start, stop, reads, writes, skip=False):
        if skip:
            P.op("pe", lambda e: e.matmul(out, lhsT=lhsT, rhs=rhs, start=start, stop=stop, skip_group_check=True),
                 reads, writes)
        else:
            P.op("pe", lambda e: e.matmul(out, lhsT=lhsT, rhs=rhs, start=start, stop=stop), reads, writes)

    def act(out, in_, func, reads, writes, **kw):
        P.op("act", lambda e: e.activation(out=out, in_=in_, func=func, **kw), reads, writes)

    def tt(out, in0, in1, op, reads, writes):
        P.op("dve", lambda e: e.tensor_tensor(out=out, in0=in0, in1=in1, op=op), reads, writes)

    def ts(out, in0, s1, s2, op0, op1, reads, writes):
        if s2 is None:
            P.op("dve", lambda e: e.tensor_scalar(out=out, in0=in0, scalar1=s1, scalar2=None, op0=op0), reads, writes)
        else:
            P.op("dve", lambda e: e.tensor_scalar(out=out, in0=in0, scalar1=s1, scalar2=s2, op0=op0, op1=op1),
                 reads, writes)

    def stt(out, in0, scalar, in1, op0, op1, reads, writes):
        P.op("dve", lambda e: e.scalar_tensor_tensor(out=out, in0=in0, scalar=scalar, in1=in1, op0=op0, op1=op1),
             reads, writes)

    def recip(out, in_, reads, writes):
        P.op("dve", lambda e: e.reciprocal(out=out, in_=in_), reads, writes)

    jt = sm[0:1, 60:64]

    def join(reads, writes):
        P.op("pool", lambda e: e.memset(jt, 0.0), reads, writes)

    P.dma("sp", cb[:], cb_d, writes=[c_r])
    P.dma("sp", cf[:], cf_d, writes=[c_r])
    P.dma("sp", cE[:], cE_d, writes=[c_r])
    for v_ in range(2):
        P.op("pool", lambda e, v_=v_: e.memset(V[v_][:], 0.0), writes=[V_r[v_]])
    P.op("pool", lambda e: e.memset(vcmp[:], 0.0), writes=[vcmp_r])
    ident = cbv("ident")
    onespad = cbv("onespad")
    m_ge, m_le, m_lt = cbv("m_ge"), cbv("m_le"), cbv("m_lt")

    def load_w(l, name, dst=None, dst_r=None):
        o, w = WL_OFF[name]
        si = nxt("wst", 2)
        P.dma("sp", wst[si][:, :, 0:w], wl_d[l, :, o:o + w].rearrange("(c p) n -> p c n", p=128), writes=[wst_r[si]])
        if dst is None:
            bi = nxt("wbf", NWB)
            dst, dst_r = wbf[bi][:, :, 0:w], wbf_r[bi]
        P.op("pool", lambda e: e.tensor_copy(out=dst, in_=wst[si][:, :, 0:w]), reads=[wst_r[si]], writes=[dst_r])
        return dst, dst_r

    def wkeep(u):
        return Fbf[1][:, u * 1024:(u + 1) * 1024].rearrange("p (c n) -> p c n", c=8)

    def proj_fm(wap, wr, M, evac, chunks=range(NCH)):
        for c in chunks:
            b = mbank()
            for kc in range(8):
                mm(pb[b][0:M, :], wap[:, kc, 0:M], hT[:, kc, c * 512:(c + 1) * 512], kc == 0, kc == 7,
                   [wr, hT_r], [pb_r[b]])
            evac(c, pb[b][0:M, :], pb_r[b])

    def proj_tm(wap, wr, N, tokview, tiles, evac):
        per = 512 // N
        tiles = list(tiles)
        for i0 in range(0, len(tiles), per):
            b = mbank()
            grp = tiles[i0:i0 + per]
            for i, m in enumerate(grp):
                for kc in range(8):
                    mm(pb[b][:, i * N:(i + 1) * N], tokview(m, kc), wap[:, kc, 0:N], kc == 0, kc == 7,
                       [wr, hT_r], [pb_r[b]])
            evac(grp[0], len(grp), pb[b], pb_r[b])

    def nat_tok(m, kc):
        return hT[:, kc, m * 128:(m + 1) * 128]

    def copy_evac(dst_fn, dst_r, func=AF.Copy):
        def f(c, ps, ps_r):
            act(dst_fn(c), ps, func, [ps_r], [dst_r])
        return f

    def vpad_evac(Vt, Vr):
        def f(m0, cnt, ps, ps_r):
            dst = Vt[:, m0:m0 + cnt, :].rearrange("p m (a b) -> p m a b", b=64)[:, :, 0:3:2, :]
            src = ps[:, 0:cnt * 128].rearrange("p (m a b) -> p m a b", a=2, b=64)
            act(dst, src, AF.Copy, [ps_r], [Vr])
        return f

    def attn_chunk(q_ap, q_r, ktiles, scale, ob, lb):
        first = True
        used = []
        for ki, kt in enumerate(ktiles):
            nk, c0, c1 = kt["nk"], kt["c0"], kt["c1"]
            for h in range(2):
                sb = sbank()
                bias = kt.get("bias")
                mm(pb[sb][0:nk, c0:c1], kt["k_ap"](h), q_ap(h)[:, c0:c1], True, bias is None,
                   [kt["k_r"], q_r], [pb_r[sb]])
                if bias is not None:
                    bl, br_, brr = bias(h)
                    mm(pb[sb][0:nk, c0:c1], bl, br_[:, c0:c1], False, True, [brr, c_r], [pb_r[sb]])
                pi = nxt("pt", NPT)
                act(pt[pi][0:nk, c0:c1], pb[sb][0:nk, c0:c1], AF.Exp, [pb_r[sb]], [pt_r[pi]], scale=scale)
                for (a, b_, map_) in kt["masks"]:
                    tt(pt[pi][0:nk, a:b_], pt[pi][0:nk, a:b_], map_, ALU.mult, [pt_r[pi], c_r], [pt_r[pi]])
                mm(pb[ob][:, c0:c1], kt["v_ap"](h), pt[pi][0:nk, c0:c1], first, False, [kt["v_r"], pt_r[pi]],
                   [pb_r[ob]], skip=True)
                mm(pb[lb][:, c0:c1], kt["ones_ap"](h), pt[pi][0:nk, c0:c1], first, False, [c_r, pt_r[pi]],
                   [pb_r[lb]], skip=True)
                first = False
                used.append((ki, h, pi))
        return used

    def ones_ap_full(h):
        return onespad[:, 0:128] if h == 0 else onespad[:, 64:192]

    def banded_ktiles(jb0, nqb, nb, k_ap_fn, k_r, v_ap_fn, v_r, tailmask):
        kts = []
        for kb in range(max(0, jb0 - nb), jb0 + nqb):
            qlo = max(kb, jb0)
            qhi = min(kb + nb, jb0 + nqb - 1)
            c0 = (qlo - jb0) * 128
            c1 = (qhi - jb0 + 1) * 128
            masks = []
            if qlo == kb:
                masks.append((c0, c0 + 128, m_ge))
            if qhi == kb + nb:
                masks.append((c1 - 128, c1, tailmask))
            kts.append(dict(nk=128, c0=c0, c1=c1, masks=masks, k_ap=(lambda h, kb=kb: k_ap_fn(kb, h)), k_r=k_r,
                            v_ap=(lambda h, kb=kb: v_ap_fn(kb, h)), v_r=v_r, ones_ap=ones_ap_full))
        return kts

    mix_r = [[R("mix%d_%d" % (i, c)) for c in range(NCH)] for i in range(8)]
    x_r = [R("x%d" % t) for t in range(NT)]

    def store_mix(ci, c, tile_ap, tile_r):
        P.dma("pool", mixT_d[ci, :, c * 512:(c + 1) * 512], tile_ap, reads=[tile_r], writes=[mix_r[ci][c]], semr=tile_r)

    def vsel(Vt, m, h):
        return Vt[:, m, 0:128] if h == 0 else Vt[:, m, 64:192]

    OB, LB = 3, 4

    selb_r = [R("selb0"), R("selb1")]

    def b_fin(T, br, c):
        s1 = nxt("sc", NSC)
        recip(sc[s1][:], pb[LB][:, :], [pb_r[LB]], [sc_r[s1]])
        tt(sc[s1][:], pb[OB][:, :], sc[s1][:], ALU.mult, [pb_r[OB], sc_r[s1]], [sc_r[s1]])
        s2 = nxt("sc", NSC)
        gcol = PF["gb"][0] + T * 3 + br
        proj_fm(wkeep(T * 3 + br), F_r[1], 128,
                (lambda c_, ps, ps_r: act(sc[s2][:], ps, AF.Sigmoid, [ps_r, pf_r], [sc_r[s2]], bias=pf[:, gcol:gcol + 1])),
                chunks=[c])
        tt(sc[s1][:], sc[s1][:], sc[s2][:], ALU.mult, [sc_r[s1], sc_r[s2]], [sc_r[s1]])
        tt(accB[T][:], accB[T][:], sc[s1][:], ALU.add, [accB_r[T], sc_r[s1]], [accB_r[T]])

    def mixer_c(l):
        w_, wr_ = load_w(l, "C_q")
        proj_fm(w_, wr_, 128, copy_evac(lambda c: G[0][:, c * 512:(c + 1) * 512], G_r[0]))
        w_, wr_ = load_w(l, "C_k")
        proj_fm(w_, wr_, 128, copy_evac(lambda c: G[1][:, c * 512:(c + 1) * 512], G_r[1]))
        G2v = G[2][:].rearrange("p (m n) -> p m n", n=128)

        def k_evac(m0, cnt, ps, ps_r):
            act(G2v[:, m0:m0 + cnt, :], ps[:, 0:cnt * 128].rearrange("p (m n) -> p m n", n=128), AF.Copy, [ps_r], [G_r[2]])
        proj_tm(w_, wr_, 128, nat_tok, range(NT), k_evac)
        cvv = Fbf[0][:].rearrange("p (m n) -> p m n", n=256)
        for half in range(2):
            w_, wr_ = load_w(l, "C_v%d" % half)

            def v_evac(m0, cnt, ps, ps_r, half=half):
                act(cvv[:, m0:m0 + cnt, half * 128:(half + 1) * 128], ps[:, 0:cnt * 128].rearrange("p (m n) -> p m n", n=128),
                    AF.Copy, [ps_r], [F_r[0]])
            proj_tm(w_, wr_, 128, nat_tok, range(NT), v_evac)
        w_, wr_ = load_w(l, "C_a")
        proj_fm(w_, wr_, 32, copy_evac(lambda c: G[3][0:32, c * 512:(c + 1) * 512], G_r[3]))
        for half in range(2):
            load_w(l, "C_z%d" % half, wkeep(half), F_r[1])
        P.op("dve", lambda e: e.tensor_copy(out=misc[0:32, 0:128], in_=pf[0:32, PF["wa2"][0]:PF["wa2"][0] + 128]),
             [pf_r], [misc_r])
        v1flat = V[1][:].rearrange("p m n -> p (m n)")
        Sst = V[1].bitcast(F32)[:].rearrange("p m n -> p (m n)")[:, 0:256]
        sst_r, stg_r = R("sst"), R("stg")
        join([V_r[1]], [sst_r, stg_r])
        P.op("dve", lambda e: e.memset(Sst, 0.0), [], [sst_r])
        Sbf = [misc[:, 256 + 256 * k:512 + 256 * k] for k in range(3)]
        sbf_r = [R("sbf%d" % k) for k in range(3)]
        join([misc_r], sbf_r)
        P.op("dve", lambda e: e.memset(Sbf[0], 0.0), [], [sbf_r[0]])
        for i in range(NMX):
            P.op("dve", lambda e, i=i: e.memset(mx[i][:, 64:192], 0.0), [], [mx_r[i]])
        stg = v1flat[:, 1024:2048]
        scur = 0
        for m in range(NT):
            tsl = slice(m * 128, (m + 1) * 128)
            b1 = mbank()
            mm(pb[b1][:, 0:128], G[3][0:32, tsl], misc[0:32, 0:128], True, True, [G_r[3], misc_r], [pb_r[b1]])
            s1 = nxt("sc", NSC)
            nla = sc[s1][:, 0:128]
            tt(nla, pb[b1][:, 0:128], prb[:, 0:128], ALU.add, [pb_r[b1], pr_r], [sc_r[s1]])
            act(nla, nla, AF.Exp, [sc_r[s1]], [sc_r[s1]], scale=-1.0)
            act(nla, nla, AF.Ln, [sc_r[s1]], [sc_r[s1]], bias=1.0)
            b2 = mbank()
            mm(pb[b2][:, 0:128], nla, cfv("tri2"), True, True, [sc_r[s1], c_r], [pb_r[b2]])
            mm(pb[b2][:, 128:256], cfv("utri2"), nla, True, True, [sc_r[s1], c_r], [pb_r[b2]])
            s2 = nxt("sc", NSC)
            act(sc[s2][:, 0:128], pb[b2][:, 0:128], AF.Exp, [pb_r[b2]], [sc_r[s2]], scale=-1.0 / 16)
            act(sc[s2][:, 128:256], pb[b2][:, 0:128], AF.Exp, [pb_r[b2]], [sc_r[s2]], scale=1.0 / 16)
            act(sc[s2][:, 256:384], pb[b2][:, 128:256], AF.Exp, [pb_r[b2]], [sc_r[s2]], scale=-1.0 / 16)
            mi = nxt("mx", NMX)
            qe2 = mx[mi][:, 0:256].rearrange("p (a b) -> p a b", b=64)[:, 0:4:3, :]
            stt(qe2, G[0][:, tsl].rearrange("p (a b) -> p a b", b=64), 32 ** -0.5,
                sc[s2][:, 0:128].rearrange("p (a b) -> p a b", b=64), ALU.mult, ALU.mult, [G_r[0], sc_r[s2]], [mx_r[mi]])
            tt(mx[mi][:, 384:512], G[1][:, tsl], sc[s2][:, 128:256], ALU.mult, [G_r[1], sc_r[s2]], [mx_r[mi]])
            rmo = CF["rm"][0]
            ts(mx[mi][:, 256:384], mx[mi][:, 384:512], cf[:, rmo:rmo + 1], None, ALU.mult, None, [mx_r[mi], c_r], [mx_r[mi]])
            ts(mx[mi][:, 384:512], mx[mi][:, 384:512], cf[:, rmo + 1:rmo + 2], None, ALU.mult, None, [mx_r[mi], c_r],
               [mx_r[mi]])
            pk = nxt("pt", NPT)
            tt(pt[pk][:, 0:128], G2v[:, m, :], sc[s2][:, 256:384], ALU.mult, [G_r[2], sc_r[s2]], [pt_r[pk]])
            kvb = mbank()
            kvb2 = sbank()
            mm(pb[kvb][:, 0:256], pt[pk][0:64, 0:128], cvv[0:64, m, :], True, True, [pt_r[pk], F_r[0]], [pb_r[kvb]])
            mm(pb[kvb2][:, 0:256], pt[pk][64:128, 0:128], cvv[64:128, m, :], True, True, [pt_r[pk], F_r[0]], [pb_r[kvb2]])
            n0 = scur
            n1 = (scur + 1) % 3
            n2 = (scur + 2) % 3
            stt(Sst, Sst, sc[s2][:, 63:64], pb[kvb][:, 0:256], ALU.mult, ALU.add, [sst_r, sc_r[s2], pb_r[kvb]], [sst_r])
            tt(Sbf[n1], Sst, cfv("zm"), ALU.mult, [sst_r, c_r], [sbf_r[n1]])
            stt(Sst, Sst, sc[s2][:, 127:128], pb[kvb2][:, 0:256], ALU.mult, ALU.add, [sst_r, sc_r[s2], pb_r[kvb2]],
                [sst_r])
            tt(Sbf[n2], Sst, cfv("zm"), ALU.mult, [sst_r, c_r], [sbf_r[n2]])
            scur = n2
            sbs = [sbank(), sbank()]
            for h in range(4):
                hs = slice(64 * (h // 2), 64 * (h // 2) + 64)
                ko = 256 + 128 * (h % 2)
                sb = sbs[h // 2]
                mm(pb[sb][:, (h % 2) * 128:(h % 2 + 1) * 128], mx[mi][hs, ko:ko + 128],
                   mx[mi][hs, 0:256].rearrange("p (a b) -> p a b", b=64)[:, 0:4:3, :], True, True, [mx_r[mi]], [pb_r[sb]])
            pi = nxt("pt", NPT)
            for hp in range(2):
                tt(pt[pi][:, hp * 256:(hp + 1) * 256], pb[sbs[hp]][:, 0:256], cbv("bdm4")[:, 0:256], ALU.mult,
                   [pb_r[sbs[hp]], c_r], [pt_r[pi]])
            ob = mbank()
            for h in range(4):
                hs = slice(64 * (h // 2), 64 * (h // 2) + 64)
                o_ = pb[ob][:, h * 64:(h + 1) * 64]
                mm(o_, pt[pi][:, h * 128:(h + 1) * 128], cvv[:, m, h * 64:(h + 1) * 64], True, False, [pt_r[pi], F_r[0]],
                   [pb_r[ob]])
                mm(o_, mx[mi][hs, 0:128], Sbf[n0][hs, h * 64:(h + 1) * 64], False, False, [mx_r[mi], sbf_r[n0]], [pb_r[ob]])
                mm(o_, mx[mi][hs, 128:256], Sbf[n1][hs, h * 64:(h + 1) * 64], False, True, [mx_r[mi], sbf_r[n1]], [pb_r[ob]])
            s3 = nxt("sc", NSC)
            o3 = sc[s3]
            act(o3[:, 0:256], pb[ob][:, 0:256], AF.Copy, [pb_r[ob]], [sc_r[s3]])
            for h in range(4):
                act(o3[:, 256:320], o3[:, h * 64:(h + 1) * 64], AF.Square, [sc_r[s3]], [sc_r[s3]],
                    accum_out=o3[:, 320 + h:321 + h])
            act(o3[:, 324:328], o3[:, 320:324], AF.Sqrt, [sc_r[s3]], [sc_r[s3]], scale=1.0 / 64, bias=EPS)
            recip(o3[:, 324:328], o3[:, 324:328], [sc_r[s3]], [sc_r[s3]])
            for h in range(4):
                ts(o3[:, h * 64:(h + 1) * 64], o3[:, h * 64:(h + 1) * 64], o3[:, 324 + h:325 + h], None, ALU.mult, None,
                   [sc_r[s3]], [sc_r[s3]])
            tt(o3[:, 0:256], o3[:, 0:256], prb[:, 128:384], ALU.mult, [sc_r[s3], pr_r], [sc_r[s3]])
            zb = mbank()
            for half in range(2):
                for kc in range(8):
                    mm(pb[zb][:, half * 128:(half + 1) * 128], hT[:, kc, tsl], wkeep(half)[:, kc, :], kc == 0, kc == 7,
                       [hT_r, F_r[1]], [pb_r[zb]])
            pz = nxt("pt", NPT)
            act(pt[pz][:, 0:256], pb[zb][:, 0:256], AF.Silu, [pb_r[zb]], [pt_r[pz]])
            tt(pt[pz][:, 256:512], o3[:, 0:256], pt[pz][:, 0:256], ALU.mult, [sc_r[s3], pt_r[pz]], [pt_r[pz]])
            tb = mbank()
            pv = pb[tb].bitcast(BF16)
            for half in range(2):
                P.op("pe", lambda e, pv=pv, half=half, pz=pz: e.transpose(
                    out=pv[:, half * 128:(half + 1) * 128], in_=pt[pz][:, 256 + half * 128:256 + (half + 1) * 128],
                    identity=ident), [pt_r[pz], c_r], [pb_r[tb]])
            for half in range(2):
                act(stg[:, half * 512 + (m % 4) * 128:half * 512 + (m % 4 + 1) * 128], pv[:, half * 128:(half + 1) * 128],
                    AF.Copy, [pb_r[tb]], [stg_r])
            if m % 4 == 3:
                store_mix(MIX_C0, m // 4, stg[:, 0:512], stg_r)
                store_mix(MIX_C1, m // 4, stg[:, 512:1024], stg_r)
        join(sbf_r, [misc_r])
        join([sst_r, stg_r], [V_r[1]])

    def out_phase(l, x_src):
        hflat = hT[:].rearrange("p c s -> p (c s)")
        wo = hflat[:, 0:8192].rearrange("p (c n) -> p c n", c=8)
        mtb = [hflat[:, 8192 + i * 4096:8192 + (i + 1) * 4096].rearrange("p (c n) -> p c n", c=8) for i in range(2)]
        hf = hT.bitcast(F32)[:].rearrange("p c s -> p (c s)")
        yb_ = [hf[:, 8192 + i * 1024:8192 + (i + 1) * 1024] for i in range(2)]
        xr = [hf[:, 10240 + i * 1024:10240 + (i + 1) * 1024] for i in range(2)]
        wo_r, mt_r, y_r, xr_r = R("wo"), [R("mt0"), R("mt1")], [R("y0"), R("y1")], [R("xr0"), R("xr1")]
        junk_r = R("junk2")
        join([hT_r, F_r[0]], [wo_r] + mt_r + y_r + xr_r + [junk_r])
        for kc in range(8):
            si = nxt("wst", 2)
            P.dma("sp", wst[si][:].rearrange("p c n -> p (c n)"), wol_d[l, kc * 128:(kc + 1) * 128, :], writes=[wst_r[si]])
            P.op("pool", lambda e, si=si, kc=kc: e.tensor_copy(out=wo[:, kc, :], in_=wst[si][:].rearrange("p c n -> p (c n)")),
                 [wst_r[si]], [wo_r])
        for c in range(NCH):
            mi = c % 2
            P.dma("pool", mtb[mi], mixT_d[:, :, c * 512:(c + 1) * 512].rearrange("c p s -> p c s"),
                  reads=[mix_r[i][c] for i in range(8)], writes=[mt_r[mi]])
            for i in range(4):
                t = 4 * c + i
                yb = t % 2
                for half in range(2):
                    b = mbank()
                    for kc in range(8):
                        mm(pb[b][:, :], mtb[mi][:, kc, i * 128:(i + 1) * 128], wo[:, kc, half * 512:(half + 1) * 512],
                           kc == 0, kc == 7, [mt_r[mi], wo_r], [pb_r[b]])
                    act(yb_[yb][:, half * 512:(half + 1) * 512], pb[b][:, :], AF.Copy, [pb_r[b]], [y_r[yb]])
                P.dma("pool", xr[yb], x_src[t * 128:(t + 1) * 128, :], reads=[x_r[t]], writes=[xr_r[yb]])
                act(Fb[0][:, 0:1024], yb_[yb], AF.Square, [y_r[yb]], [junk_r, sm_r], accum_out=sm[:, 42 + yb:43 + yb])
                act(sm[:, 44 + yb:45 + yb], sm[:, 42 + yb:43 + yb], AF.Sqrt, [sm_r], [sm_r], scale=1.0 / D, bias=EPS)
                recip(sm[:, 46 + yb:47 + yb], sm[:, 44 + yb:45 + yb], [sm_r], [sm_r])
                stt(yb_[yb], yb_[yb], sm[:, 46 + yb:47 + yb], gp_t[:], ALU.mult, ALU.mult, [y_r[yb], sm_r, gp_r], [y_r[yb]])
                tt(xr[yb], xr[yb], yb_[yb], ALU.add, [xr_r[yb], y_r[yb]], [xr_r[yb]])
                P.dma("pool", out_d[t * 128:(t + 1) * 128, :], xr[yb], reads=[xr_r[yb]], writes=[x_r[t]], semr=xr_r[yb])
        join([wo_r] + mt_r + y_r + xr_r + [junk_r], [hT_r, F_r[0]])

    for l in range(nl):
        x_src = x_in if l == 0 else out_d
        P.dma("sp", pf[:], pf_d[l], writes=[pf_r])
        P.dma("sp", prb[:, 0:128], pr_d[l, :, PR["ba"][0]:PR["ba"][0] + 128].to_broadcast([128, 128]), writes=[pr_r])
        P.dma("sp", prb[:, 128:384], pr_d[l, :, PR["glanorm"][0]:PR["glanorm"][0] + 256].to_broadcast([128, 256]),
              writes=[pr_r])
        P.dma("sp", sm[:, 0:8], cT_d, writes=[sm_r])
        act(sm[:, 0:8], sm[:, 0:8], AF.Silu, [sm_r], [sm_r])
        wa_t = Fb[0][:].rearrange("p (c n) -> p c n", c=8)
        for blk in range(4):
            P.dma("sp", wa_t, wada_d[l, :, blk * 512:(blk + 1) * 512].rearrange("(c p) n -> p c n", p=128),
                  writes=[F_r[0]])
            b = mbank()
            for j in range(4):
                for kc in range(8):
                    mm(pb[b][:, j:j + 1], wa_t[:, kc, j * 128:(j + 1) * 128], sm[:, kc:kc + 1], kc == 0, kc == 7,
                       [F_r[0], sm_r], [pb_r[b]])
            bo = PF["bada"][0] + blk * 4
            tt(sm[:, 8 + blk * 4:12 + blk * 4], pb[b][:, 0:4], pf[:, bo:bo + 4], ALU.add, [pb_r[b], pf_r, sm_r], [sm_r])
        stt(sm[:, 24:32], sm[:, 16:24], 1.0, pfv("lnpre"), ALU.add, ALU.mult, [sm_r, pf_r], [sm_r])
        scbt = Fb[1][:, 0:1024].rearrange("p (c n) -> p c n", c=8)
        for kc in range(8):
            act(scbt[:, kc, :], cfv("ones"), AF.Copy, [sm_r, c_r], [F_r[1]], scale=sm[:, kc:kc + 1])
        P.dma("sp", Fb[1][:, 2048:3072], pr_d[l, :, PR["lnpost"][0]:PR["lnpost"][0] + 1024].to_broadcast([128, 1024]),
              writes=[F_r[1]])
        P.dma("sp", Fb[1][:, 3072:4096], pr_d[l, :, PR["badag"][0]:PR["badag"][0] + 1024].to_broadcast([128, 1024]),
              writes=[F_r[1]])
        for half in range(2):
            P.dma("sp", wa_t, wada_d[l, :, 2048 + half * 512:2048 + (half + 1) * 512].rearrange("(c p) n -> p c n", p=128),
                  writes=[F_r[0]])
            b = mbank()
            for kc in range(8):
                mm(pb[b][:, :], scbt[:, kc, :], wa_t[:, kc, :], kc == 0, kc == 7, [F_r[0], F_r[1]], [pb_r[b]])
            tt(gp_t[:, half * 512:(half + 1) * 512], pb[b][:, :], Fb[1][:, 3072 + half * 512:3072 + (half + 1) * 512],
               ALU.add, [pb_r[b], F_r[1]], [gp_r])
        tt(gp_t[:], gp_t[:], Fb[1][:, 2048:3072], ALU.mult, [gp_r, F_r[1]], [gp_r])
        GT = Fb[1][:, 0:1024].rearrange("p (c n) -> p c n", c=8)
        ST = Fb[1][:, 1024:2048].rearrange("p (c n) -> p c n", c=8)
        for kc in range(8):
            act(GT[:, kc, :], cfv("ones"), AF.Copy, [sm_r, c_r], [F_r[1]], scale=sm[:, 24 + kc:25 + kc])
            act(ST[:, kc, :], cfv("ones"), AF.Copy, [sm_r, c_r], [F_r[1]], scale=sm[:, 8 + kc:9 + kc])

        xt = [Fb[0][:, 0:1024], Fb[0][:, 1024:2048]]
        xt_r = [R("xt0"), R("xt1")]
        junk = Fb[0][:, 2048:3072]
        junk_r = R("junk")
        xnb = [G[0][:, 0:1024], G[0][:, 1024:2048]]
        xn_r = [R("xn0"), R("xn1")]
        tmpN = Fb[1][:, 2048:3072]
        tmp_r = R("tmpN")
        gst_r = R("gst")
        join([F_r[0], G_r[0], F_r[1]], xt_r + [junk_r] + xn_r + [tmp_r, gst_r])
        for t in range(NT):
            xb = t % 2
            P.dma("pool", xt[xb], x_src[t * 128:(t + 1) * 128, :], reads=[x_r[t]], writes=[xt_r[xb]])
            act(junk, xt[xb], AF.Square, [xt_r[xb]], [junk_r, sm_r], accum_out=sm[:, 32 + xb:33 + xb])
            act(sm[:, 34 + xb:35 + xb], sm[:, 32 + xb:33 + xb], AF.Sqrt, [sm_r], [sm_r], scale=1.0 / D, bias=EPS)
            recip(sm[:, 36 + xb:37 + xb], sm[:, 34 + xb:35 + xb], [sm_r], [sm_r])
            ts(xnb[xb], xt[xb], sm[:, 36 + xb:37 + xb], None, ALU.mult, None, [sm_r, xt_r[xb]], [xn_r[xb]])
            b = mbank()
            pv = pb[b].bitcast(BF16)
            for kc in range(8):
                P.op("pe", lambda e, pv=pv, kc=kc, xb=xb: e.transpose(out=pv[:, kc * 128:(kc + 1) * 128],
                                                                     in_=xnb[xb][:, kc * 128:(kc + 1) * 128],
                                                                     identity=ident),
                     [xn_r[xb], c_r], [pb_r[b]])
            tt(tmpN, pv[:], Fb[1][:, 0:1024], ALU.mult, [pb_r[b], gst_r], [tmp_r])
            tt(hT[:, :, t * 128:(t + 1) * 128], tmpN.rearrange("p (c n) -> p c n", c=8), ST, ALU.add,
               [tmp_r, gst_r], [hT_r])
        join(xt_r + [junk_r] + xn_r + [tmp_r, gst_r], [F_r[0], G_r[0], F_r[1]])

        if "A" in mixers:
            for p in range(2):
                w_, wr_ = load_w(l, "A_q%d" % p)
                proj_fm(w_, wr_, 128, copy_evac(lambda c: G[0][:, c * 512:(c + 1) * 512], G_r[0]))
                w_, wr_ = load_w(l, "A_k%d" % p)
                proj_fm(w_, wr_, 128, copy_evac(lambda c: G[1][:, c * 512:(c + 1) * 512], G_r[1]))
                w_, wr_ = load_w(l, "A_z%d" % p)
                proj_fm(w_, wr_, 128, copy_evac(lambda c: G[2][:, c * 512:(c + 1) * 512], G_r[2], AF.Silu))
                for r in (1, 4, 16):
                    nblk = S // (128 * r)
                    w_, wr_ = load_w(l, "A_v%d" % p)

                    def tokview(m, kc, r=r, nblk=nblk):
                        rho, jb = divmod(m, nblk)
                        st = rho + r * 128 * jb
                        return hT[:, kc, st:st + r * 127 + 1:r]
                    proj_tm(w_, wr_, 128, tokview, range(NT), vpad_evac(V[0], V_r[0]))
                    for rho in range(r):
                        for jb0 in range(0, nblk, 4):
                            nqb = min(4, nblk - jb0)
                            Nq = nqb * 128
                            st = rho + r * 128 * jb0
                            sl = slice(st, st + r * (Nq - 1) + 1, r)
                            kts = banded_ktiles(
                                jb0, nqb, 1,
                                (lambda kb, h, rho=rho, r=r: G[1][h * 64:(h + 1) * 64,
                                                                   rho + r * 128 * kb:rho + r * 128 * kb + r * 127 + 1:r]),
                                G_r[1],
                                (lambda kb, h, rho=rho, nblk=nblk: vsel(V[0], rho * nblk + kb, h)),
                                V_r[0], m_le)
                            attn_chunk(lambda h, sl=sl: G[0][h * 64:(h + 1) * 64, sl], G_r[0], kts, 0.125, OB, LB)
                            if r == 1:
                                act(Fb[0][:, sl], pb[OB][:, 0:Nq], AF.Copy, [pb_r[OB]], [F_r[0]])
                                P.op("dve", lambda e, sl=sl, Nq=Nq: e.tensor_copy(out=Fb[1][:, sl], in_=pb[LB][:, 0:Nq]),
                                     [pb_r[LB]], [F_r[1]])
                            else:
                                tt(Fb[0][:, sl], pb[OB][:, 0:Nq], Fb[0][:, sl], ALU.add, [pb_r[OB], F_r[0]], [F_r[0]])
                                tt(Fb[1][:, sl], pb[LB][:, 0:Nq], Fb[1][:, sl], ALU.add, [pb_r[LB], F_r[1]], [F_r[1]])
                for c in range(NCH):
                    cs = slice(c * 512, (c + 1) * 512)
                    recip(Fb[1][:, cs], Fb[1][:, cs], [F_r[1]], [F_r[1]])
                    tt(Fb[0][:, cs], Fb[0][:, cs], Fb[1][:, cs], ALU.mult, [F_r[0], F_r[1]], [F_r[0]])
                    mi = nxt("mx", NMX)
                    tt(mx[mi][:], Fb[0][:, cs], G[2][:, cs], ALU.mult, [F_r[0], G_r[2]], [mx_r[mi]])
                    store_mix(MIX_A0 + p, c, mx[mi][:], mx_r[mi])

        if "D" in mixers:
            for p in range(2):
                w_, wr_ = load_w(l, "D_q%d" % p)
                proj_fm(w_, wr_, 128, copy_evac(lambda c, p=p: G[p][:, c * 512:(c + 1) * 512], G_r[p]))
            w_, wr_ = load_w(l, "D_k")
            proj_fm(w_, wr_, 128, copy_evac(lambda c: G[2][:, c * 512:(c + 1) * 512], G_r[2]))
            w_, wr_ = load_w(l, "D_v")

            def dv_evac(m0, cnt, ps, ps_r):
                src = ps[:, 0:cnt * 64].rearrange("p (m b) -> p m b", b=64)
                act(V[0][:, m0:m0 + cnt, 0:64], src, AF.Copy, [ps_r], [V_r[0]])
                act(V[0][:, m0:m0 + cnt, 128:192], src, AF.Copy, [ps_r], [V_r[0]])
            proj_tm(w_, wr_, 64, nat_tok, range(NT), dv_evac)
            for p in range(2):
                load_w(l, "D_z%d" % p, wkeep(p), F_r[1])
            act(sm[:, 38:40], pfv("sink"), AF.Exp, [pf_r], [sm_r])
            for p in range(2):
                for c in range(NCH):
                    kts = banded_ktiles(4 * c, 4, 1,
                                        (lambda kb, h: G[2][h * 64:(h + 1) * 64, kb * 128:(kb + 1) * 128]), G_r[2],
                                        (lambda kb, h: vsel(V[0], kb, h)), V_r[0], m_lt)
                    attn_chunk(lambda h, p=p, c=c: G[p][h * 64:(h + 1) * 64, c * 512:(c + 1) * 512], G_r[p], kts,
                               0.125, OB, LB)
                    s1 = nxt("sc", NSC)
                    ts(sc[s1][:], pb[LB][:, :], sm[:, 38 + p:39 + p], None, ALU.add, None, [pb_r[LB], sm_r], [sc_r[s1]])
                    recip(sc[s1][:], sc[s1][:], [sc_r[s1]], [sc_r[s1]])
                    tt(sc[s1][:], pb[OB][:, :], sc[s1][:], ALU.mult, [pb_r[OB], sc_r[s1]], [sc_r[s1]])
                    pz = nxt("pt", NPT)
                    proj_fm(wkeep(p), F_r[1], 128, copy_evac(lambda c_, pz=pz: pt[pz][:], pt_r[pz], AF.Silu), chunks=[c])
                    mi = nxt("mx", NMX)
                    tt(mx[mi][:], sc[s1][:], pt[pz][:], ALU.mult, [sc_r[s1], pt_r[pz]], [mx_r[mi]])
                    store_mix(MIX_D0 + p, c, mx[mi][:], mx_r[mi])

        if "B" in mixers:
            P.op("pool", lambda e: e.memset(V[1][:], 0.0), [], [V_r[1]])
            for nm, gi in (("B_qX", 0), ("B_qY", 1), ("B_ks", 2), ("B_kw", 3)):
                w_, wr_ = load_w(l, nm)
                proj_fm(w_, wr_, 128, copy_evac(lambda c, gi=gi: G[gi][:, c * 512:(c + 1) * 512], G_r[gi]))
            KC = Fbf[0][:, 0:4096]
            VC = Fbf[0][:, 4096:8192]
            w_, wr_ = load_w(l, "B_kc")
            proj_fm(w_, wr_, 128, copy_evac(lambda c: KC[:, c * 512:(c + 1) * 512], F_r[0]))
            w_, wr_ = load_w(l, "B_vc")
            proj_fm(w_, wr_, 128, copy_evac(lambda c: VC[:, c * 512:(c + 1) * 512], F_r[0]))
            w_, wr_ = load_w(l, "B_vs")
            proj_tm(w_, wr_, 128, nat_tok, range(NT), vpad_evac(V[0], V_r[0]))
            w_, wr_ = load_w(l, "B_vw")
            proj_tm(w_, wr_, 128, nat_tok, range(NT), vpad_evac(V[1], V_r[1]))
            for u, nm in enumerate(("B_gX0", "B_gX1", "B_gX2", "B_gY0", "B_gY1", "B_gY2", "B_zX", "B_zY")):
                load_w(l, nm, wkeep(u), F_r[1])
            P.op("dve", lambda e: e.tensor_copy(out=misc[:, 0:256], in_=pf[:, PF["w2k"][0]:PF["w2k"][0] + 256]),
                 [pf_r], [misc_r])
            P.op("dve", lambda e: e.tensor_copy(out=misc[:, 256:320], in_=pf[:, PF["posk"][0]:PF["posk"][0] + 64]),
                 [pf_r], [misc_r])
            for kvi, kv in enumerate(("k", "v")):
                src = KC if kv == "k" else VC
                hb = [mbank(), mbank()]
                cbk = sbank()
                for lg in range(4):
                    si = nxt("wst", 2)
                    P.dma("sp", wst[si][:], w1_d[kv][l, :, lg * 8:(lg + 1) * 8, :], writes=[wst_r[si]])
                    bi = nxt("wbf", NWB)
                    P.op("pool", lambda e, si=si, bi=bi: e.tensor_copy(out=wbf[bi][:], in_=wst[si][:]),
                         [wst_r[si]], [wbf_r[bi]])
                    for li in range(8):
                        ll = lg * 8 + li
                        for g in range(2):
                            mm(pb[hb[g]][:, 0:255], wbf[bi][g * 64:(g + 1) * 64, li, :],
                               src[g * 64:(g + 1) * 64, ll:ll + 16 * 254 + 1:16], ll == 0, ll == 31,
                               [wbf_r[bi], F_r[0]], [pb_r[hb[g]]])
                        mm(pb[cbk][:, 0:1], wbf[bi][0:64, li, :], misc[0:64, 256 + 32 * kvi + ll:257 + 32 * kvi + ll],
                           ll == 0, ll == 31, [wbf_r[bi], misc_r], [pb_r[cbk]])
                bcol = PF["b1k"][0] + kvi
                tt(sm[:, 40:41], pb[cbk][:, 0:1], pf[:, bcol:bcol + 1], ALU.add, [pb_r[cbk], pf_r], [sm_r])
                for g in range(2):
                    s1, s2 = nxt("sc", NSC), nxt("sc", NSC)
                    u_ = sc[s1][:, 0:255]
                    t_ = sc[s2][:, 0:255]
                    ts(u_, pb[hb[g]][:, 0:255], sm[:, 40:41], None, ALU.add, None, [pb_r[hb[g]], sm_r], [sc_r[s1]])
                    tt(t_, u_, u_, ALU.mult, [sc_r[s1]], [sc_r[s2]])
                    ts(t_, t_, 0.044715, 1.0, ALU.mult, ALU.add, [sc_r[s2]], [sc_r[s2]])
                    tt(t_, t_, u_, ALU.mult, [sc_r[s1], sc_r[s2]], [sc_r[s2]])
                    act(t_, t_, AF.Sigmoid, [sc_r[s2]], [sc_r[s2]], scale=1.5957691216057308)
                    gi_ = nxt("pt", NPT)
                    tt(pt[gi_][:, 0:255], u_, t_, ALU.mult, [sc_r[s1], sc_r[s2]], [pt_r[gi_]])
                    if kv == "k":
                        b = mbank()
                        mm(pb[b][:, 0:255], misc[:, 0:128], pt[gi_][:, 0:255], True, True, [misc_r, pt_r[gi_]], [pb_r[b]])
                        act(kcmpT[g * 64:(g + 1) * 64, 0:255], pb[b][g * 64:(g + 1) * 64, 0:255], AF.Copy,
                            [pb_r[b]], [kcmp_r])
                    else:
                        b = mbank()
                        for kt_ in range(2):
                            nk = 128 if kt_ == 0 else 127
                            mm(pb[b][0:nk, kt_ * 64:(kt_ + 1) * 64], pt[gi_][:, kt_ * 128:kt_ * 128 + nk],
                               misc[:, 128:192], True, True, [misc_r, pt_r[gi_]], [pb_r[b]])
                        for kt_ in range(2):
                            nk = 128 if kt_ == 0 else 127
                            act(vcmp[0:nk, kt_, g * 128:g * 128 + 64], pb[b][0:nk, kt_ * 64:(kt_ + 1) * 64], AF.Copy,
                                [pb_r[b]], [vcmp_r])
            join([F_r[0]], accB_r + impS_r)
            for c in range(NCH):
                qsl = slice(c * 512, (c + 1) * 512)
                ub = {}
                for T in range(2):
                    kts = []
                    for kt_ in range(2):
                        if kt_ == 1 and c < 4:
                            continue
                        nk = 128 if kt_ == 0 else 127
                        masks = []
                        so = CB["stair"][0]
                        if kt_ == 0 and c < 5:
                            masks.append((0, 512, cb[0:nk, so + 512 * c:so + 512 * c + 512]))
                        if kt_ == 1:
                            masks.append((0, 512, cb[0:nk, so + 512 * (c - 4):so + 512 * (c - 4) + 512]))
                        kts.append(dict(
                            nk=nk, c0=0, c1=512, masks=masks,
                            k_ap=(lambda h, kt_=kt_, nk=nk: kcmpT[h * 64:(h + 1) * 64, kt_ * 128:kt_ * 128 + nk]),
                            k_r=kcmp_r,
                            v_ap=(lambda h, kt_=kt_, nk=nk: (vcmp[0:nk, kt_, 0:128] if h == 0 else vcmp[0:nk, kt_, 64:192])),
                            v_r=vcmp_r,
                            ones_ap=(lambda h, nk=nk: (onespad[0:nk, 0:128] if h == 0 else onespad[0:nk, 64:192])),
                            kt=kt_))
                    used = attn_chunk(lambda h, T=T: G[T][h * 64:(h + 1) * 64, qsl], G_r[T], kts, 0.125, OB, LB)
                    banks = [mbank(), mbank()]
                    ub[T] = banks
                    ovo = CB["ov1"][0]
                    for qb in range(4):
                        for h in range(2):
                            lst = [(ki, pi) for (ki, hh, pi) in used if hh == h]
                            col = ((qb % 2) * 2 + h) * 65
                            for n_, (ki, pi) in enumerate(lst):
                                nk = kts[ki]["nk"]
                                kt_ = kts[ki]["kt"]
                                mm(pb[banks[qb // 2]][:, col:col + 65], pt[pi][0:nk, qb * 128:(qb + 1) * 128],
                                   cb[0:nk, ovo + kt_ * 65:ovo + (kt_ + 1) * 65], n_ == 0, n_ == len(lst) - 1,
                                   [pt_r[pi], c_r], [pb_r[banks[qb // 2]]])
                    for qb in range(4):
                        for h in range(2):
                            col = ((qb % 2) * 2 + h) * 65
                            bk = banks[qb // 2]
                            ts(sm[:, 48:49], pb[bk][:, col + 64:col + 65], 1e-30, None, ALU.max, None, [pb_r[bk], sm_r], [sm_r])
                            recip(sm[:, 49:50], sm[:, 48:49], [sm_r], [sm_r])
                            ts(impS[T][:, (qb * 2 + h) * 64:(qb * 2 + h + 1) * 64], pb[bk][:, col:col + 64], sm[:, 49:50], None,
                               ALU.mult, None, [pb_r[bk], sm_r], [impS_r[T]])
                    s1 = nxt("sc", NSC)
                    ts(sc[s1][:], pb[LB][:, :], 1e-30, None, ALU.max, None, [pb_r[LB]], [sc_r[s1]])
                    recip(sc[s1][:], sc[s1][:], [sc_r[s1]], [sc_r[s1]])
                    tt(sc[s1][:], pb[OB][:, :], sc[s1][:], ALU.mult, [pb_r[OB], sc_r[s1]], [sc_r[s1]])
                    s2 = nxt("sc", NSC)
                    gcol = PF["gb"][0] + T * 3 + 0
                    proj_fm(wkeep(T * 3 + 0), F_r[1], 128,
                            (lambda c_, ps, ps_r, s2=s2, gcol=gcol: act(sc[s2][:], ps, AF.Sigmoid, [ps_r, pf_r], [sc_r[s2]],
                                                                        bias=pf[:, gcol:gcol + 1])), chunks=[c])
                    tt(accB[T][:], sc[s1][:], sc[s2][:], ALU.mult, [sc_r[s1], sc_r[s2]], [accB_r[T]])
                for qb in range(4):
                    qbg = 4 * c + qb
                    for g in range(2):
                        s1 = nxt("sc", NSC)
                        w = sc[s1]
                        io = (qb * 2 + g) * 64
                        tt(w[:, 0:64], impS[0][:, io:io + 64], impS[1][:, io:io + 64], ALU.add, impS_r, [sc_r[s1]])
                        po = CF["patC"][0] + 64 - 2 * qbg
                        pa = CF["patA"][0] + 64 - 2 * qbg
                        tt(w[:, 0:64], w[:, 0:64], cf[:, po:po + 64], ALU.mult, [sc_r[s1], c_r], [sc_r[s1]])
                        tt(w[:, 0:64], w[:, 0:64], cf[:, pa:pa + 64], ALU.add, [sc_r[s1], c_r], [sc_r[s1]])
                        P.op("dve", lambda e, w=w: e.memset(w[:, 0:1], 1e4), [sc_r[s1]], [sc_r[s1]])
                        P.op("dve", lambda e, w=w: e.max(out=w[:, 128:136], in_=w[:, 0:64]), [sc_r[s1]], [sc_r[s1]])
                        P.op("dve", lambda e, w=w: e.match_replace(out=w[:, 136:200], in_to_replace=w[:, 128:136],
                                                                  in_values=w[:, 0:64], imm_value=-3e4),
                             [sc_r[s1]], [sc_r[s1]])
                        P.op("dve", lambda e, w=w: e.max(out=w[:, 128:136], in_=w[:, 136:200]), [sc_r[s1]], [sc_r[s1]])
                        P.op("dve", lambda e, w=w: e.match_replace(out=w[:, 136:200], in_to_replace=w[:, 128:136],
                                                                  in_values=w[:, 136:200], imm_value=-3e4),
                             [sc_r[s1]], [sc_r[s1]])
                        tt(w[:, 200:264], w[:, 0:64], w[:, 136:200], ALU.subtract, [sc_r[s1]], [sc_r[s1]])
                        ts(w[:, 200:264], w[:, 200:264], 1.0, None, ALU.min, None, [sc_r[s1]], [sc_r[s1]])
                        mi = nxt("mx", NMX)
                        ts(mx[mi][:, g * 64:g * 64 + 64], w[:, 200:264], 30000.0, -30000.0, ALU.mult, ALU.add, [sc_r[s1]],
                           [mx_r[mi]])
                        if g == 1:
                            P.op("dve", lambda e, mi=mi: e.memset(mx[mi][:, 0:64], 0.0), [mx_r[mi]], [mx_r[mi]])
                        nr = 64 * (g + 1)
                        b = mbank()
                        pv = pb[b].bitcast(BF16)
                        P.op("pe", lambda e, pv=pv, mi=mi, nr=nr: e.transpose(out=pv[0:nr, 0:128], in_=mx[mi][:, 0:nr],
                                                                              identity=ident), [mx_r[mi], c_r], [pb_r[b]])
                        act(misc[g * 64:g * 64 + 64, 320 + g * 512 + qb * 128:320 + g * 512 + (qb + 1) * 128],
                            pv[g * 64:g * 64 + 64, 0:128], AF.Copy, [pb_r[b]], [selb_r[g]])
                for T in range(2):
                    kts = []
                    for kb in range(4 * c + 4):
                        r_ = kb - 4 * c
                        c0 = 0 if r_ < 0 else 128 * r_
                        masks = [] if r_ < 0 else [(c0, c0 + 128, m_ge)]
                        kts.append(dict(
                            nk=128, c0=c0, c1=512, masks=masks,
                            k_ap=(lambda h, kb=kb: G[2][h * 64:(h + 1) * 64, kb * 128:(kb + 1) * 128]), k_r=G_r[2],
                            v_ap=(lambda h, kb=kb: vsel(V[0], kb, h)), v_r=V_r[0], ones_ap=ones_ap_full,
                            bias=(lambda h, kb=kb: (cE[h * 64:h * 64 + 64, kb * 128:(kb + 1) * 128],
                                                    misc[h * 64:h * 64 + 64, 320 + h * 512:320 + (h + 1) * 512],
                                                    selb_r[h]))))
                    attn_chunk(lambda h, T=T: G[T][h * 64:(h + 1) * 64, qsl], G_r[T], kts, 0.125, OB, LB)
                    b_fin(T, 1, c)
                for T in range(2):
                    kts = banded_ktiles(4 * c, 4, 4,
                                        (lambda kb, h: G[3][h * 64:(h + 1) * 64, kb * 128:(kb + 1) * 128]), G_r[3],
                                        (lambda kb, h: vsel(V[1], kb, h)), V_r[1], m_lt)
                    attn_chunk(lambda h, T=T: G[T][h * 64:(h + 1) * 64, qsl], G_r[T], kts, 0.125, OB, LB)
                    b_fin(T, 2, c)
                for T in range(2):
                    pz = nxt("pt", NPT)
                    proj_fm(wkeep(6 + T), F_r[1], 128, copy_evac(lambda c_, pz=pz: pt[pz][:], pt_r[pz], AF.Silu), chunks=[c])
                    mi = nxt("mx", NMX)
                    tt(mx[mi][:], accB[T][:], pt[pz][:], ALU.mult, [accB_r[T], pt_r[pz]], [mx_r[mi]])
                    store_mix(MIX_BX + T, c, mx[mi][:], mx_r[mi])

            join(accB_r + impS_r, [F_r[0]])
        if "C" in mixers:
            mixer_c(l)

        out_phase(l, x_src)

    if debug:
        for i in range(8):
            P.dma("pool", dbg_d[i], mixT_d[i], reads=[mix_r[i][c] for c in range(NCH)], semr=mix_r[i][0])
    P.emit()
    return nc, P


_CACHE = {}


def _run_layers(inp, nl_prog):
    lay = _host_layouts(inp)
    cb, cf, cE = _consts()
    nl = lay['wl'].shape[0]
    B = inp['x'].shape[0]
    if nl_prog not in _CACHE:
        _CACHE[nl_prog] = build_program(nl_prog)[0]
    nc = _CACHE[nl_prog]
    xs = [np.ascontiguousarray(inp['x'][b]) for b in range(B)]
    for l0 in range(0, nl, nl_prog):
        sl = slice(l0, l0 + nl_prog)
        in_maps = []
        for b in range(B):
            in_maps.append(dict(x=xs[b], cT=lay['cT'][b], wl=lay['wl'][sl], wol=lay['wol'][sl], wada=lay['wada'][sl],
                                pf=lay['pf'][sl], pr=lay['pr'][sl], w1k=lay['w1k'][sl], w1v=lay['w1v'][sl],
                                cb=cb, cf=cf, cE=cE))
        res = run_bass_kernel_spmd(nc, in_maps, core_ids=list(range(B)))
        xs = [np.asarray(res.results[b]["out"], dtype=np.float32) for b in range(B)]
    return np.stack(xs, axis=0)


def kernel(**inputs):
    inp = {k: np.asarray(v) for k, v in inputs.items()}
    return _run_layers(inp, 1)
```
